# Optimizing a Trainium2 kernel written in Bass

```python
import jax, jax.numpy as jnp
from jax import lax
import numpy as np

D_MODEL = 2048
BATCH = 4
SEQ = 4096
DEPTH = 1

D_FF = 5632
FFN_RES_SCALE = 0.5
FOURIER_WIDTH = D_MODEL // 2
FOURIER_GROUPS = 4
FOURIER_GROUP_DIM = FOURIER_WIDTH // FOURIER_GROUPS
LRU_WIDTH = D_MODEL // 2
LRU_HEADS = 8
LRU_HEAD_DIM = LRU_WIDTH // LRU_HEADS
LRU_C = 8.0
N_DIRECTIONS = 2
CONV_WIDTH = 4
CONV_PAD_LEFT = 2
CONV_PAD_RIGHT = CONV_WIDTH - 1 - CONV_PAD_LEFT
N_BRANCHES = 2
IN_WIDTH = FOURIER_WIDTH + 2 * LRU_WIDTH + N_BRANCHES * D_MODEL
RMS_EPS = 1e-6

kernel_name = 'hybrid_fnet_rglru_macaron_encoder'


def rms_norm(x, g):
    xf = x.astype(jnp.float32)
    y = xf * lax.rsqrt(jnp.mean(xf * xf, axis=-1, keepdims=True) + RMS_EPS)
    return (y * g.astype(jnp.float32)).astype(x.dtype)


def swiglu(h, w_gate, w_up, w_down):
    return (jax.nn.silu(h @ w_gate) * (h @ w_up)) @ w_down


def fourier_mix(z):
    b, s, _ = z.shape
    zg = z.astype(jnp.float32).reshape(b, s, FOURIER_GROUPS, FOURIER_GROUP_DIM)
    y = jnp.fft.fft2(zg, axes=(1, 3), norm='ortho').real
    return y.reshape(b, s, FOURIER_WIDTH).astype(z.dtype)


def centred_dwconv(v, w, bias):
    s = v.shape[1]
    vp = jnp.pad(v, ((0, 0), (CONV_PAD_LEFT, CONV_PAD_RIGHT), (0, 0)))
    out = vp[:, 0:s] * w[0]
    for k in range(1, CONV_WIDTH):
        out = out + vp[:, k:k + s] * w[k]
    return out + bias


def _combine(left, right):
    a_l, b_l = left
    a_r, b_r = right
    return a_l * a_r, a_r * b_l + b_r


def linear_scan(a, u, reverse):
    _, h = lax.associative_scan(_combine, (a, u), axis=1, reverse=reverse)
    return h


def rglru_bidir(v, wa, ba, wx, bx, lam):
    b, s, _ = v.shape
    vh = v.reshape(b, s, LRU_HEADS, LRU_HEAD_DIM)
    r = jax.nn.sigmoid((jnp.einsum('bshi,dhij->dbshj', vh, wa) + ba[:, None, None]).astype(jnp.float32))
    i = jax.nn.sigmoid((jnp.einsum('bshi,dhij->dbshj', vh, wx) + bx[:, None, None]).astype(jnp.float32))
    log_a = -LRU_C * r * jax.nn.softplus(-lam.astype(jnp.float32))[:, None, None]
    a = jnp.exp(log_a)
    u = jnp.sqrt(-jnp.expm1(2.0 * log_a)) * i * vh.astype(jnp.float32)[None]
    h = linear_scan(a[0], u[0], reverse=False) + linear_scan(a[1], u[1], reverse=True)
    return h.reshape(b, s, LRU_WIDTH).astype(v.dtype)


def setup_inputs(seed: int = 0) -> dict:
    key = jax.random.key(seed)
    ks = jax.random.split(key, 24)
    L, D, F = DEPTH, D_MODEL, D_FF
    H, Hd = LRU_HEADS, LRU_HEAD_DIM

    def nrm(k, shape, fan_in):
        return jax.random.normal(k, shape, jnp.float32) * (fan_in ** -0.5)

    def gain(k, shape):
        return 1.0 + 0.01 * jax.random.normal(k, shape, jnp.float32)

    def bias(k, shape):
        return 0.01 * jax.random.normal(k, shape, jnp.float32)

    a_c = jax.random.uniform(ks[14], (L, N_DIRECTIONS, H, Hd), jnp.float32, 0.9, 0.999)
    s_lam = a_c ** (1.0 / LRU_C)
    lru_lambda = jnp.log(s_lam) - jnp.log1p(-s_lam)

    return {
        'x': jax.random.normal(ks[0], (BATCH, SEQ, D), jnp.float32),
        'ffn1_norm': gain(ks[1], (L, D)),
        'ffn1_w_gate': nrm(ks[2], (L, D, F), D),
        'ffn1_w_up': nrm(ks[3], (L, D, F), D),
        'ffn1_w_down': nrm(ks[4], (L, F, D), F),
        'mix_norm': gain(ks[5], (L, D)),
        'w_in': nrm(ks[6], (L, D, IN_WIDTH), D),
        'b_gates': bias(ks[7], (L, N_BRANCHES, D)),
        'conv_w': nrm(ks[8], (L, CONV_WIDTH, LRU_WIDTH), CONV_WIDTH),
        'conv_b': bias(ks[9], (L, LRU_WIDTH)),
        'lru_wa': nrm(ks[10], (L, N_DIRECTIONS, H, Hd, Hd), Hd),
        'lru_ba': bias(ks[11], (L, N_DIRECTIONS, H, Hd)),
        'lru_wx': nrm(ks[12], (L, N_DIRECTIONS, H, Hd, Hd), Hd),
        'lru_bx': bias(ks[13], (L, N_DIRECTIONS, H, Hd)),
        'lru_lambda': lru_lambda,
        'proj_a': nrm(ks[15], (L, FOURIER_WIDTH, D), FOURIER_WIDTH),
        'proj_b': nrm(ks[16], (L, LRU_WIDTH, D), LRU_WIDTH),
        'w_out': nrm(ks[17], (L, D, D), D),
        'ffn2_norm': gain(ks[18], (L, D)),
        'ffn2_w_gate': nrm(ks[19], (L, D, F), D),
        'ffn2_w_up': nrm(ks[20], (L, D, F), D),
        'ffn2_w_down': nrm(ks[21], (L, F, D), F),
        'final_norm': gain(ks[22], (D,)),
    }


def reference(x, ffn1_norm, ffn1_w_gate, ffn1_w_up, ffn1_w_down, mix_norm, w_in, b_gates,
              conv_w, conv_b, lru_wa, lru_ba, lru_wx, lru_bx, lru_lambda, proj_a, proj_b, w_out,
              ffn2_norm, ffn2_w_gate, ffn2_w_up, ffn2_w_down, final_norm):
    b, s, _ = x.shape
    split_points = [FOURIER_WIDTH, FOURIER_WIDTH + LRU_WIDTH, FOURIER_WIDTH + 2 * LRU_WIDTH]
    for l in range(DEPTH):
        h = rms_norm(x, ffn1_norm[l])
        x = x + FFN_RES_SCALE * swiglu(h, ffn1_w_gate[l], ffn1_w_up[l], ffn1_w_down[l])

        u = rms_norm(x, mix_norm[l])
        z = u @ w_in[l]
        z_four, z_rec, z_gelu, z_gates = jnp.split(z, split_points, axis=-1)
        gates = jax.nn.sigmoid(z_gates.reshape(b, s, N_BRANCHES, D_MODEL) + b_gates[l])

        y_a = fourier_mix(z_four) @ proj_a[l]

        v = centred_dwconv(z_rec, conv_w[l], conv_b[l])
        y_rec = rglru_bidir(v, lru_wa[l], lru_ba[l], lru_wx[l], lru_bx[l], lru_lambda[l])
        y_b = (y_rec * jax.nn.gelu(z_gelu)) @ proj_b[l]

        merged = gates[:, :, 0, :] * y_a + gates[:, :, 1, :] * y_b
        x = x + merged @ w_out[l]

        h = rms_norm(x, ffn2_norm[l])
        x = x + FFN_RES_SCALE * swiglu(h, ffn2_w_gate[l], ffn2_w_up[l], ffn2_w_down[l])
    return rms_norm(x, final_norm)
```

```python
import math
from contextlib import ExitStack

import numpy as np
import ml_dtypes
import concourse.bass as bass
import concourse.mybir as mybir
from concourse.bass_utils import run_bass_kernel_spmd

F32 = mybir.dt.float32
BF16 = mybir.dt.bfloat16
AF = mybir.ActivationFunctionType
ALU = mybir.AluOpType

D = 2048
FF = 5632
SEQ = 4096
NTOK = 2048
T = 512
NT = NTOK // T
NG = FF // 256
EPS = 1e-6
NPV = 132
RG = [[0, 1], [2, 3], [4, 5], [6, 7]]
ARENA_BYTES = 206 * 1024
SEG = 8192
NSEG = 6
NDS = 12
DFT_BYTES = 118784 + 256

DEBUG = False
PHASES = (1, 2, 3, 4)


class Res:
    __slots__ = ("name", "w", "r", "wd", "rd")

    def __init__(self, name):
        self.name = name
        self.w = {}
        self.r = {}
        self.wd = []
        self.rd = []


class Op:
    __slots__ = ("eng", "fn", "deps", "kind", "mark", "k", "semi", "val")


class Prog:
    ENG = ("pe", "act", "dve", "pool", "sp")

    def __init__(self):
        self.ops = {e: [] for e in self.ENG}
        self.dma_rr = {"sp": 0, "pool": 0}
        self.dma_cnt = {}
        self.dma_last = {}
        self.all_dma = []
        self.ncc = 0
        self.bypass = False
        self.barrier_deps = {e: [] for e in self.ENG}

    def add(self, eng, fn, reads=(), writes=(), kind="c"):
        op = Op()
        op.eng, op.fn, op.kind, op.mark, op.k = eng, fn, kind, False, -1
        raw = []
        oth = []
        for R in reads:
            raw += list(R.w.values()) + R.wd
        for R in writes:
            oth += list(R.w.values()) + R.wd + list(R.r.values()) + R.rd
        if not self.bypass:
            oth += self.barrier_deps[eng]
            self.barrier_deps[eng] = []
        if kind == "cc":
            op.semi = ("cc", self.ncc)
            self.ncc += 1
            op.val = 1
            self.all_dma.append(op)
        elif kind != "c":
            i = self.dma_rr[eng]
            self.dma_rr[eng] = (i + 1) % NDS
            prev = self.dma_last.get((eng, i))
            if prev is not None:
                oth.append(prev)
            inc = 16 if kind == "dma" else 1
            op.semi = (eng, i)
            op.val = self.dma_cnt.get((eng, i), 0) + inc
            self.dma_cnt[(eng, i)] = op.val
            self.dma_last[(eng, i)] = op
            self.all_dma.append(op)
        deps = []
        seen = set()
        for lst, is_raw in ((raw, True), (oth, False)):
            for d in lst:
                if d is op or id(d) in seen:
                    continue
                if d.kind == "c" and d.eng == eng and eng == "pe":
                    continue
                seen.add(id(d))
                deps.append(d)
                if d.kind == "c":
                    d.mark = True
        op.deps = deps
        for R in reads:
            if kind == "c":
                R.r[eng] = op
            else:
                R.rd.append(op)
        for R in writes:
            if kind == "c":
                R.w[eng] = op
            else:
                R.wd.append(op)
        self.ops[eng].append(op)
        return op

    def barrier(self, final=False):
        deps = []
        for e in ("pe", "act", "dve"):
            if self.ops[e]:
                o = self.ops[e][-1]
                o.mark = True
                deps.append(o)
        deps += [d for d in self.all_dma if final or d.kind != "cc"]
        self.all_dma = [] if final else [d for d in self.all_dma if d.kind == "cc"]
        for e in self.ENG:
            self.barrier_deps[e] = self.barrier_deps[e] + list(deps)

    def finalize(self):
        self.nmarks = {}
        for e in self.ENG:
            k = 0
            for op in self.ops[e]:
                if op.kind == "c" and op.mark:
                    op.k = k
                    k += 1
            self.nmarks[e] = k
            assert k < SEG * NSEG, (e, k)

    def emit(self, eng, e, sems):
        waited = {}
        for op in self.ops[eng]:
            need = {}
            for d in op.deps:
                if d.kind == "c":
                    key = ("c", d.eng, d.k // SEG)
                    val = d.k % SEG + 1
                else:
                    key = ("d",) + d.semi
                    val = d.val
                if waited.get(key, 0) >= val:
                    continue
                if need.get(key, 0) < val:
                    need[key] = val
            for key, val in need.items():
                e.wait_ge(sems[key], val)
                waited[key] = val
            ins = op.fn(e)
            if ins is None:
                continue
            if op.kind == "c":
                if op.mark:
                    ins.then_inc(sems[("c", eng, op.k // SEG)], 1)
            elif op.kind == "dma":
                ins.then_inc(sems[("d",) + op.semi], 16)
            else:
                ins.then_inc(sems[("d",) + op.semi])


class Arena:
    def __init__(self, big):
        self.big = big
        self.off = 0

    def alloc(self, free_shape, dtype, at=None):
        esz = 4 if dtype == F32 else 2
        n = 1
        for s in free_shape:
            n *= s
        if at is None:
            off = (self.off + 63) // 64 * 64
            self.off = off + n * esz
            self.last = off
        else:
            off = at
        nbytes = n * esz
        assert off + nbytes <= ARENA_BYTES, ("arena overflow", off + nbytes)
        ap = self.big[:, off // 2: off // 2 + nbytes // 2]
        if dtype == F32:
            ap = ap.bitcast(F32)
        if len(free_shape) == 2:
            ap = ap.rearrange("p (a b) -> p a b", a=free_shape[0])
        elif len(free_shape) == 3:
            ap = ap.rearrange("p (a b c) -> p a b c", a=free_shape[0], b=free_shape[1])
        elif len(free_shape) == 4:
            ap = ap.rearrange("p (a b c d) -> p a b c d", a=free_shape[0], b=free_shape[1], c=free_shape[2])
        return ap


def rev_ap(ap, n):
    a = ap.ap
    assert len(a) == 2 and a[1][0] == 1 and a[1][1] == n, a
    return bass.AP(ap.tensor, ap.offset + (n - 1), [[a[0][0], a[0][1]], [-1, n]])


def build_program():
    nc = bass.Bass("TRN2", target_bir_lowering=False)
    P = Prog()

    def din(name, shape, dt):
        return nc.dram_tensor(name, shape, dt, kind="ExternalInput").ap()

    xin = din("x", [NTOK, D], F32)
    f1 = din("f1", [NG, 128, 12288], F32)
    f2 = din("f2", [NG, 128, 12288], F32)
    win = din("win", [14, 128, 8192], F32)
    pa = din("pa", [4, 128, 4096], F32)
    pb = din("pb", [4, 128, 4096], F32)
    wo = din("wo", [4, 128, 8192], F32)
    lruw = din("lruw", [128, 4096], F32)
    gam = din("gam", [128, 4 * D], F32)
    pv = din("pv", [128, NPV], F32)
    dfts = din("dfts", [8, 128, 16384], BF16)
    dftc = din("dftc", [128, 1024], BF16)
    ident = din("ident", [128, 128], BF16)
    out = nc.dram_tensor("out", [NTOK, D], F32, kind="ExternalOutput").ap()

    dbg = "ExternalOutput" if DEBUG else "Internal"
    x1s = nc.dram_tensor("x1s", [NTOK, D], F32, kind=dbg).ap()
    zrs = nc.dram_tensor("zrs", [128, 8, NTOK], F32, kind=dbg).ap()
    yts = nc.dram_tensor("yts", [128, 8, NTOK], BF16, kind=dbg).ap()
    hss = nc.dram_tensor("hss", [128, 8, NTOK], BF16, kind=dbg).ap()
    x2s = nc.dram_tensor("x2s", [NTOK, D], F32, kind=dbg).ap() if DEBUG else None
    zfx = [nc.dram_tensor(f"zfx{t}", [T, 1024], BF16).ap() for t in range(NT)]
    zfg = [nc.dram_tensor(f"zfg{t}", [2 * T, 1024], BF16).ap() for t in range(NT)]
    hxi = nc.dram_tensor("hxi", [128, 64], F32).ap()
    hxo = nc.dram_tensor("hxo", [256, 64], F32).ap()
    cxi = nc.dram_tensor("cxi", [128, 64], F32).ap()
    cxo = nc.dram_tensor("cxo", [256, 64], F32).ap()
    cxi_b = nc.dram_tensor("cxi_b", [128, 64], F32).ap()
    cxo_b = nc.dram_tensor("cxo_b", [256, 64], F32).ap()

    R_x1s, R_zrs, R_yts, R_hss = Res("x1s"), Res("zrs"), Res("yts"), Res("hss")
    R_zfx = [Res(f"zfx{t}") for t in range(NT)]
    R_zfg = [Res(f"zfg{t}") for t in range(NT)]
    R_hxi, R_hxo, R_cxi, R_cxo = Res("hxi"), Res("hxo"), Res("cxi"), Res("cxo")
    R_out = Res("out")
    R_none = Res("none")

    with ExitStack() as es:
        big = es.enter_context(nc.sbuf_tensor("arena", [128, ARENA_BYTES // 2], BF16))
        NPS = 6
        psb = [es.enter_context(nc.psum_tensor(f"ps{i}", [128, 512], F32)) for i in range(NPS)]
        pst = [es.enter_context(nc.psum_tensor(f"pst{i}", [128, 1024], BF16)) for i in range(2)]
        AR = Arena(big)
        R_ps = [Res(f"ps{i}") for i in range(NPS)]
        R_pst = [Res("pst0"), Res("pst1")]
        ps_rr = [0]
        pst_rr = [0]

        def psum():
            i = ps_rr[0]
            ps_rr[0] = (i + 1) % NPS
            return psb[i][:], R_ps[i]

        ident_sb = AR.alloc([128], BF16)
        gbuf = [AR.alloc([D], F32) for _ in range(2)]
        g_res = [None, None]
        R_gamma = [Res("gam0"), Res("gam1")]
        g_rr = [0]
        PH = [1]
        pv_sb = AR.alloc([NPV], F32)
        eps_sb = AR.alloc([1], F32)
        R_const = Res("const")
        P.add("sp", lambda e: e.dma_start(out=ident_sb, in_=ident[:, :]), writes=[R_const], kind="dma")
        P.add("sp", lambda e: e.dma_start(out=pv_sb, in_=pv[:, :]), writes=[R_const], kind="dma")
        P.add("dve", lambda e: e.memset(eps_sb, EPS), writes=[R_const])
        base_off = AR.off

        class Ring:
            def __init__(self, n, elems):
                self.n = n
                self.slots = [AR.alloc([elems], BF16) for _ in range(n)]
                self.res = [Res(f"ring{i}") for i in range(n)]
                self.i = 0

            def load(self, src, elems):
                i = self.i
                self.i = (i + 1) % self.n
                slot, r = self.slots[i], self.res[i]
                P.add("pool", lambda e: e.dma_start(out=slot[:, 0:elems], in_=src, max_dma_last_dim=8192),
                      writes=[r], kind="dma")
                return slot, r

        def load_gamma(gidx):
            i = 0 if gidx in (0, 1) and not (gidx == 1 and g_res[0] == 0 and g_res[1] in (None, 1) and PH[0] == 1) else 1
            if PH[0] == 1:
                i = gidx
            else:
                i = 0 if gidx == 1 else 1
            if g_res[i] != gidx:
                g_res[i] = gidx
                P.add("sp", lambda e: e.dma_start(out=gbuf[i], in_=gam[:, gidx * D:(gidx + 1) * D]),
                      writes=[R_gamma[i]], kind="dma")
            return gbuf[i], R_gamma[i]

        def mm(out_ap, lhsT, rhs, start, stop, reads, writes):
            P.add("pe", lambda e: e.matmul(out=out_ap, lhsT=lhsT, rhs=rhs, start=start, stop=stop),
                  reads=reads, writes=writes)

        def norm_transpose(xs, R_xs, gidx, hT, R_hT, tmp):
            ss, R_ss, rstd, R_rstd, hn, R_hn = tmp
            gsb, R_g = load_gamma(gidx)
            for i in range(4):
                b = i % 2
                P.add("act", lambda e, i=i, b=b: e.activation(out=hn[:, b, :], in_=xs[:, i, :], func=AF.Square,
                                                              accum_out=ss[:, i:i + 1]),
                      reads=R_xs[i], writes=[R_hn[b], R_ss[i]])
                P.add("act", lambda e, i=i: e.activation(out=rstd[:, i:i + 1], in_=ss[:, i:i + 1], func=AF.Sqrt,
                                                         scale=1.0 / D, bias=eps_sb),
                      reads=[R_ss[i], R_const], writes=[R_rstd[i]])
                P.add("dve", lambda e, i=i: e.reciprocal(out=rstd[:, i:i + 1], in_=rstd[:, i:i + 1]),
                      reads=[R_rstd[i]], writes=[R_rstd[i]])
                P.add("dve", lambda e, i=i, b=b: e.scalar_tensor_tensor(
                    out=hn[:, b, :], in0=xs[:, i, :], scalar=rstd[:, i:i + 1], in1=gsb,
                    op0=ALU.mult, op1=ALU.mult),
                    reads=R_xs[i] + [R_rstd[i], R_g], writes=[R_hn[b]])
                for q in range(4):
                    h = pst_rr[0]
                    pst_rr[0] = 1 - h
                    tp = pst[h][:, 0:512]
                    for a in range(4):
                        kc = q * 4 + a
                        P.add("pe", lambda e, tp=tp, a=a, kc=kc, b=b: e.transpose(
                            out=tp[:, a * 128:(a + 1) * 128], in_=hn[:, b, kc * 128:(kc + 1) * 128],
                            identity=ident_sb),
                            reads=[R_hn[b], R_const], writes=[R_pst[h]])
                    dst = hT[:, q * 4:(q + 1) * 4, i * 128:(i + 1) * 128]
                    src = tp.rearrange("p (a b) -> p a b", a=4)
                    if q % 2 == 0:
                        P.add("act", lambda e, dst=dst, src=src: e.activation(out=dst, in_=src, func=AF.Copy),
                              reads=[R_pst[h]], writes=[R_hT[i][q]])
                    else:
                        P.add("dve", lambda e, dst=dst, src=src: e.tensor_copy(out=dst, in_=src),
                              reads=[R_pst[h]], writes=[R_hT[i][q]])

        def RH(R_hT, kc, i=None):
            if i is None:
                return [R_hT[j][kc // 4] for j in range(4)]
            return [R_hT[i][kc // 4]]

        def ffn(xs, R_xs, hT, R_hT, wsrc, ring, sg, R_sg, actb, R_actb):
            prev = None
            for g in range(NG + 1):
                if g < NG:
                    slot, R_slot = ring.load(wsrc[g], 12288)
                    ab = g % 2
                    for j in range(2):
                        gp, R_gp = psum()
                        up, R_up = psum()
                        for kc in range(16):
                            mm(gp, slot[:, kc * 256 + j * 128: kc * 256 + j * 128 + 128], hT[:, kc, :],
                               kc == 0, kc == 15, [R_slot] + RH(R_hT, kc), [R_gp])
                        for kc in range(16):
                            o = 4096 + kc * 256 + j * 128
                            mm(up, slot[:, o:o + 128], hT[:, kc, :], kc == 0, kc == 15, [R_slot] + RH(R_hT, kc), [R_up])
                        sb = (g * 2 + j) % 2
                        P.add("act", lambda e, gp=gp, sb=sb: e.activation(out=sg[:, sb, :], in_=gp, func=AF.Silu),
                              reads=[R_gp], writes=[R_sg[sb]])
                        P.add("dve", lambda e, up=up, sb=sb, ab=ab, j=j: e.tensor_tensor(
                            out=actb[:, ab, j, :], in0=sg[:, sb, :], in1=up, op=ALU.mult),
                            reads=[R_sg[sb], R_up], writes=[R_actb[ab]])
                if prev is not None:
                    pslot, R_pslot, pab = prev
                    for i in range(4):
                        for fb in range(4):
                            dp, R_dp = psum()
                            for j in range(2):
                                o = 8192 + j * 2048 + fb * 512
                                mm(dp, actb[:, pab, j, i * 128:(i + 1) * 128], pslot[:, o:o + 512],
                                   j == 0, j == 1, [R_actb[pab], R_pslot], [R_dp])
                            P.add("dve", lambda e, dp=dp, i=i, fb=fb: e.scalar_tensor_tensor(
                                out=xs[:, i, fb * 512:(fb + 1) * 512], in0=dp, scalar=0.5,
                                in1=xs[:, i, fb * 512:(fb + 1) * 512], op0=ALU.mult, op1=ALU.add),
                                reads=[R_dp, R_xs[i][fb]], writes=[R_xs[i][fb]])
                prev = (slot, R_slot, ab) if g < NG else None

        def phase1():
            AR.off = base_off
            ring = Ring(3, 12288)
            xsb = [AR.alloc([4, D], F32) for _ in range(2)]
            R_xsb = [[[Res(f"xs{k}_{i}_{f}") for f in range(4)] for i in range(4)] for k in range(2)]
            hT = AR.alloc([16, T], BF16)
            R_hT = [[Res(f"hT{i}_{q}") for q in range(4)] for i in range(4)]
            sg = AR.alloc([2, T], F32)
            sg_off = AR.last
            R_sg = [Res("sg0"), Res("sg1")]
            actb = AR.alloc([2, 2, T], BF16)
            R_actb = [Res("act0"), Res("act1")]
            ss = AR.alloc([4], F32)
            rstd = AR.alloc([4], F32)
            hn = AR.alloc([2, D], BF16)
            ntmp = (ss, [Res(f"ss{i}") for i in range(4)], rstd, [Res(f"rstd{i}") for i in range(4)],
                    hn, [Res("hn0"), Res("hn1")])
            zf_sb = AR.alloc([4, 512], BF16)
            R_zf = Res("zf_sb")
            zr_sb = AR.alloc([8, T], F32)
            R_zr = Res("zr_sb")

            def load_x(t):
                for i in range(4):
                    P.add("sp", lambda e, i=i: e.dma_start(
                        out=xsb[t % 2][:, i, :], in_=xin[t * T + i * 128:t * T + (i + 1) * 128, :]),
                        writes=R_xsb[t % 2][i], kind="dma")

            hal1 = AR.alloc([64], F32)
            R_hal1 = Res("hal1")
            P.add("dve", lambda e: e.memset(hal1, 0.0), writes=[R_hal1])
            load_x(0)
            load_gamma(0)
            load_gamma(1)
            for t in range(NT):
                xs, R_xs = xsb[t % 2], R_xsb[t % 2]
                norm_transpose(xs, R_xs, 0, hT, R_hT, ntmp)
                if t + 1 < NT:
                    load_x(t + 1)
                ffn(xs, R_xs, hT, R_hT, f1, ring, sg, R_sg, actb, R_actb)
                norm_transpose(xs, R_xs, 1, hT, R_hT, ntmp)
                P.add("sp", lambda e, t=t, xs=xs: e.dma_start(
                    out=x1s[t * T:(t + 1) * T, :].rearrange("(i p) d -> p i d", p=128), in_=xs),
                    reads=sum(R_xs, []), writes=[R_x1s], kind="dma")
                for cb in range(2):
                    slot, R_slot = ring.load(win[cb], 8192)
                    for i in range(4):
                        zp, R_zp = psum()
                        for kc in range(16):
                            mm(zp, hT[:, kc, i * 128:(i + 1) * 128], slot[:, kc * 512:(kc + 1) * 512],
                               kc == 0, kc == 15, RH(R_hT, kc, i) + [R_slot], [R_zp])
                        P.add("act", lambda e, zp=zp, i=i: e.activation(
                            out=zf_sb[:, i, :], in_=zp, func=AF.Copy),
                            reads=[R_zp], writes=[R_zf])
                    P.add("sp", lambda e, t=t, cb=cb: e.dma_start(
                        out=zfx[t][:, cb * 512:(cb + 1) * 512].rearrange("(i p) c -> p i c", p=128), in_=zf_sb),
                        reads=[R_zf], writes=[R_zfx[t]], kind="dma")
                P.add("pool", lambda e, t=t: e.collective_compute(
                    "AllGather", ALU.bypass, replica_groups=RG, ins=[zfx[t].opt()], outs=[zfg[t].opt()]),
                    reads=[R_zfx[t]], writes=[R_zfg[t]], kind="cc")
                for u in range(2):
                    slot, R_slot = ring.load(win[2 + u], 8192)
                    for c4 in range(4):
                        mc = u * 4 + c4
                        zp, R_zp = psum()
                        for kc in range(16):
                            o = kc * 512 + c4 * 128
                            mm(zp, slot[:, o:o + 128], hT[:, kc, :], kc == 0, kc == 15, RH(R_hT, kc) + [R_slot], [R_zp])
                        P.add("dve", lambda e, zp=zp, mc=mc: e.tensor_copy(out=zr_sb[:, mc, :], in_=zp),
                              reads=[R_zp], writes=[R_zr])
                P.add("sp", lambda e, t=t: e.dma_start(out=zrs[:, :, t * T:(t + 1) * T], in_=zr_sb),
                      reads=[R_zr], writes=[R_zrs], kind="dma")
                if t == NT - 1:
                    P.add("dve", lambda e: e.tensor_copy(
                        out=hal1[:, 0:16].rearrange("p (m k) -> p m k", m=8), in_=zr_sb[:, :, T - 2:T]),
                        reads=[R_zr, R_hal1], writes=[R_hal1])
                    P.add("sp", lambda e: e.dma_start(out=hxi[:, :], in_=hal1), reads=[R_hal1], writes=[R_hxi],
                          kind="dma")
                    P.add("pool", lambda e: e.collective_compute(
                        "AllGather", ALU.bypass, replica_groups=RG, ins=[hxi.opt()], outs=[hxo.opt()]),
                        reads=[R_hxi], writes=[R_hxo], kind="cc")
            P.barrier()

        def phase2a():
            AR.off = base_off
            NP = NTOK + 4
            zr = AR.alloc([8, NP], F32)
            R_zrm = [Res(f"zr{m}") for m in range(8)]
            vbf = AR.alloc([8, NTOK], BF16)
            R_v = [Res(f"v{m}") for m in range(8)]
            lw = AR.alloc([4096], BF16)
            R_lw = Res("lw")
            tmpb = [[AR.alloc([NTOK], F32) for _ in range(4)] for _ in range(2)]
            R_tmp = [[Res(f"tmp{k}{q}") for q in range(4)] for k in range(2)]
            hsb = AR.alloc([2, NTOK], BF16)
            R_hsb = [Res("hs0"), Res("hs1")]
            small = AR.alloc([768], F32)
            R_small = Res("small")
            R_cc = Res("cc")
            R_carry = Res("carry")
            R_cin = Res("cin")
            R_halo = Res("halo")
            hal_in = small[:, 0:64]
            hal_bo = small[:, 64:192].rearrange("p (r c) -> p r c", r=2)
            hal_p = small[:, 192:208]
            car_in = [small[:, 208:272], small[:, 480:544]]
            car_bo = [small[:, 272:400].rearrange("p (r c) -> p r c", r=2),
                      small[:, 544:672].rearrange("p (r c) -> p r c", r=2)]
            cxi2 = [cxi, cxi_b]
            cxo2 = [cxo, cxo_b]
            R_cxi2 = [R_cxi, Res("cxi_b")]
            R_cxo2 = [R_cxo, Res("cxo_b")]
            R_car2 = [Res("carry0"), Res("carry1")]
            R_cin2 = [Res("cin0"), Res("cin1")]
            cinit = small[:, 400:408]
            lx = small[:, 416:432]
            lt = small[:, 432:448]
            cc1 = small[:, 448:464]
            cc2 = small[:, 464:480]

            P.add("pool", lambda e: e.dma_start(out=lw, in_=lruw[:, :], max_dma_last_dim=8192),
                  writes=[R_lw], kind="dma")
            for m in range(8):
                P.add("sp", lambda e, m=m: e.dma_start(out=zr[:, m, 2:2 + NTOK], in_=zrs[:, m, :]),
                      reads=[R_zrs], writes=[R_zrm[m]], kind="dma")
            P.add("dve", lambda e: e.memset(zr[:, :, 0:2], 0.0), writes=R_zrm)
            P.add("dve", lambda e: e.memset(small, 0.0), writes=[R_small, R_halo, R_cc] + R_car2 + R_cin2)
            lam = pv_sb[:, 112:128]
            P.add("act", lambda e: e.activation(out=lx, in_=lam, func=AF.Exp, scale=-1.0),
                  reads=[R_const, R_small], writes=[R_cc])
            P.add("dve", lambda e: e.tensor_scalar(out=lt, in0=lx, scalar1=-0.25, scalar2=1.0 / 3.0,
                                                   op0=ALU.mult, op1=ALU.add), reads=[R_cc], writes=[R_cc])
            P.add("dve", lambda e: e.tensor_tensor(out=lt, in0=lt, in1=lx, op=ALU.mult), reads=[R_cc], writes=[R_cc])
            P.add("dve", lambda e: e.tensor_scalar(out=lt, in0=lt, scalar1=-1.0, scalar2=0.5,
                                                   op0=ALU.mult, op1=ALU.add), reads=[R_cc], writes=[R_cc])
            P.add("dve", lambda e: e.tensor_tensor(out=lt, in0=lt, in1=lx, op=ALU.mult), reads=[R_cc], writes=[R_cc])
            P.add("dve", lambda e: e.tensor_scalar(out=lt, in0=lt, scalar1=-1.0, scalar2=1.0,
                                                   op0=ALU.mult, op1=ALU.add), reads=[R_cc], writes=[R_cc])
            P.add("dve", lambda e: e.tensor_tensor(out=lt, in0=lt, in1=lx, op=ALU.mult), reads=[R_cc], writes=[R_cc])
            P.add("dve", lambda e: e.tensor_scalar(out=cc1, in0=lt, scalar1=-8.0, scalar2=None, op0=ALU.mult),
                  reads=[R_cc], writes=[R_cc])
            P.add("dve", lambda e: e.tensor_scalar(out=cc2, in0=lt, scalar1=-16.0, scalar2=None, op0=ALU.mult),
                  reads=[R_cc], writes=[R_cc])
            P.add("sp", lambda e: e.dma_start(out=hal_bo, in_=hxo.rearrange("(r p) c -> p r c", p=128)),
                  reads=[R_hxo], writes=[R_halo], kind="dma")
            sel0 = pv_sb[:, 128:129]
            sel1 = pv_sb[:, 129:130]
            P.add("dve", lambda e: e.tensor_scalar(out=hal_p, in0=hal_bo[:, 0, 0:16], scalar1=sel0, scalar2=None,
                                                   op0=ALU.mult), reads=[R_halo, R_const], writes=[R_halo])
            P.add("dve", lambda e: e.scalar_tensor_tensor(out=hal_p, in0=hal_bo[:, 1, 0:16], scalar=sel1, in1=hal_p,
                                                          op0=ALU.mult, op1=ALU.add),
                  reads=[R_halo, R_const], writes=[R_halo])
            hp3 = hal_p.rearrange("p (m k) -> p m k", m=8)
            P.add("dve", lambda e: e.tensor_copy(out=zr[:, :, NTOK + 2:NTOK + 3], in_=hp3[:, :, 1:2]),
                  reads=[R_halo], writes=R_zrm)
            P.add("dve", lambda e: e.tensor_copy(out=zr[:, :, NTOK + 3:NTOK + 4], in_=hp3[:, :, 0:1]),
                  reads=[R_halo], writes=R_zrm)
            for m in range(8):
                ctmp = tmpb[m % 2][0]
                R_ct = R_tmp[m % 2][0]
                P.add("dve", lambda e, m=m, ctmp=ctmp: e.tensor_scalar(
                    out=ctmp, in0=zr[:, m, 0:NTOK], scalar1=pv_sb[:, 32 + m * 5:33 + m * 5],
                    scalar2=pv_sb[:, 72 + m:73 + m], op0=ALU.mult, op1=ALU.add),
                    reads=[R_zrm[m], R_const], writes=[R_ct])
                for d in range(1, 5):
                    o = ctmp if d < 4 else vbf[:, m, :]
                    P.add("dve", lambda e, m=m, d=d, o=o, ctmp=ctmp: e.scalar_tensor_tensor(
                        out=o, in0=zr[:, m, d:d + NTOK], scalar=pv_sb[:, 32 + m * 5 + d:33 + m * 5 + d], in1=ctmp,
                        op0=ALU.mult, op1=ALU.add),
                        reads=[R_zrm[m], R_const, R_ct], writes=[R_ct] if d < 4 else [R_v[m]])
            it = 0
            for dr in range(2):
                for m in range(8):
                    k = it % 2
                    it += 1
                    rr, ii, aa, gg = tmpb[k]
                    R_rr, R_ii, R_aa, R_gg = R_tmp[k]
                    for tb in range(4):
                        for wsel, dstb, R_d, boff in ((0, rr, R_rr, 80), (1, ii, R_ii, 96)):
                            zp, R_zp = psum()
                            wo_ = ((wsel * 2 + dr) * 8 + m) * 128
                            mm(zp, lw[:, wo_:wo_ + 128], vbf[:, m, tb * 512:(tb + 1) * 512], True, True,
                               [R_lw, R_v[m]], [R_zp])
                            P.add("act", lambda e, zp=zp, dstb=dstb, tb=tb, bo=boff + dr * 8 + m: e.activation(
                                out=dstb[:, tb * 512:(tb + 1) * 512], in_=zp, func=AF.Sigmoid,
                                bias=pv_sb[:, bo:bo + 1]),
                                reads=[R_zp, R_const], writes=[R_d])
                    ci = dr * 8 + m
                    P.add("act", lambda e, aa=aa, rr=rr, ci=ci: e.activation(out=aa, in_=rr, func=AF.Exp,
                                                                             scale=cc1[:, ci:ci + 1]),
                          reads=[R_rr, R_cc], writes=[R_aa])
                    P.add("dve", lambda e, gg=gg, aa=aa: e.tensor_tensor(out=gg, in0=aa, in1=aa, op=ALU.mult),
                          reads=[R_aa], writes=[R_gg])
                    P.add("act", lambda e, gg=gg: e.activation(out=gg, in_=gg, func=AF.Sqrt, scale=-1.0, bias=1.0),
                          reads=[R_gg], writes=[R_gg])
                    P.add("dve", lambda e, ii=ii, m=m: e.tensor_tensor(out=ii, in0=ii, in1=vbf[:, m, :], op=ALU.mult),
                          reads=[R_ii, R_v[m]], writes=[R_ii])
                    P.add("dve", lambda e, ii=ii, gg=gg: e.tensor_tensor(out=ii, in0=ii, in1=gg, op=ALU.mult),
                          reads=[R_ii, R_gg], writes=[R_ii])
                    if dr == 0:
                        P.add("dve", lambda e, aa=aa, ii=ii, m=m: e.tensor_tensor_scan(
                            out=zr[:, m, 0:NTOK], data0=aa, data1=ii, initial=0.0, op0=ALU.mult, op1=ALU.add),
                            reads=[R_aa, R_ii], writes=[R_zrm[m]])
                        if m % 4 == 3:
                            hh = m // 4
                            P.add("dve", lambda e, hh=hh: e.tensor_copy(
                                out=car_in[hh][:, 0:4], in_=zr[:, hh * 4:hh * 4 + 4, NTOK - 1]),
                                reads=R_zrm[hh * 4:hh * 4 + 4] + [R_small], writes=[R_car2[hh]])
                            P.add("sp", lambda e, hh=hh: e.dma_start(out=cxi2[hh][:, :], in_=car_in[hh]),
                                  reads=[R_car2[hh]], writes=[R_cxi2[hh]], kind="dma")
                            P.add("pool", lambda e, hh=hh: e.collective_compute(
                                "AllGather", ALU.bypass, replica_groups=RG, ins=[cxi2[hh].opt()],
                                outs=[cxo2[hh].opt()]),
                                reads=[R_cxi2[hh]], writes=[R_cxo2[hh]], kind="cc")
                            P.add("sp", lambda e, hh=hh: e.dma_start(
                                out=car_bo[hh], in_=cxo2[hh].rearrange("(r p) c -> p r c", p=128)),
                                reads=[R_cxo2[hh]], writes=[R_car2[hh]], kind="dma")
                    else:
                        if m % 4 == 0:
                            hh = m // 4
                            ci_ = cinit[:, hh * 4:hh * 4 + 4]
                            P.add("dve", lambda e, hh=hh, ci_=ci_: e.tensor_scalar(
                                out=ci_, in0=car_bo[hh][:, 0, 0:4], scalar1=sel0, scalar2=None, op0=ALU.mult),
                                reads=[R_car2[hh], R_const], writes=[R_cin2[hh]])
                            P.add("dve", lambda e, hh=hh, ci_=ci_: e.scalar_tensor_tensor(
                                out=ci_, in0=car_bo[hh][:, 1, 0:4], scalar=sel1, in1=ci_, op0=ALU.mult,
                                op1=ALU.add),
                                reads=[R_car2[hh], R_const, R_cin2[hh]], writes=[R_cin2[hh]])
                        P.add("dve", lambda e, aa=aa, ii=ii, rr=rr, m=m: e.tensor_tensor_scan(
                            out=rev_ap(rr, NTOK), data0=rev_ap(aa, NTOK), data1=rev_ap(ii, NTOK),
                            initial=cinit[:, m:m + 1], op0=ALU.mult, op1=ALU.add),
                            reads=[R_aa, R_ii, R_cin2[m // 4]], writes=[R_rr])
                        hb = m % 2
                        P.add("dve", lambda e, rr=rr, m=m, hb=hb: e.tensor_tensor(
                            out=hsb[:, hb, :], in0=zr[:, m, 0:NTOK], in1=rr, op=ALU.add),
                            reads=[R_rr, R_zrm[m]], writes=[R_hsb[hb]])
                        P.add("sp", lambda e, m=m, hb=hb: e.dma_start(out=hss[:, m, :], in_=hsb[:, hb, :]),
                              reads=[R_hsb[hb]], writes=[R_hss], kind="dma")
            P.barrier()

        def phase2b():
            AR.off = base_off
            zs = AR.alloc([32, 512], BF16)
            R_zs = Res("zs")
            tbl = [AR.alloc([32, 512], BF16) for _ in range(2)]
            R_tbl = [Res("tbl0"), Res("tbl1")]
            absb = AR.alloc([2, 2, 4, 512], BF16)
            R_ab = [[Res(f"ab{b}{g}") for g in range(2)] for b in range(2)]
            ctab = AR.alloc([2, 2, 256], BF16)
            R_ctab = Res("ctab")
            ytb = AR.alloc([2, 512], BF16)
            R_ytb = [Res("yt0"), Res("yt1")]
            assert AR.off - base_off <= DFT_BYTES, AR.off - base_off
            P.add("sp", lambda e: e.dma_start(out=ctab, in_=dftc.rearrange("p (a b m) -> p a b m", a=2, b=2)),
                  writes=[R_ctab], kind="dma")
            pas = 0
            yti = 0
            for ch in range(2):
                for t in range(NT):
                    P.add("pool", lambda e, t=t, ch=ch: e.dma_start(
                        out=zs[:, t * 8:(t + 1) * 8, :],
                        in_=zfg[t][:, ch * 512:(ch + 1) * 512].rearrange("(s p) c -> p s c", p=128)),
                        reads=[R_zfg[t]], writes=[R_zs], kind="dma")
                for kb in range(4):
                    bufi = (ch * 4 + kb) % 2
                    for tr in range(2):
                        tb_ = tbl[pas % 2]
                        R_tb = R_tbl[pas % 2]
                        pas += 1
                        P.add("pool", lambda e, tb_=tb_, kb=kb, tr=tr: e.dma_start(
                            out=tb_, in_=dfts[kb * 2 + tr].rearrange("p (s k) -> p s k", s=32)),
                            writes=[R_tb], kind="dma")
                        for cc in range(4):
                            ap_, R_ap = psum()
                            for sc in range(32):
                                mm(ap_, zs[:, sc, cc * 128:(cc + 1) * 128], tb_[:, sc, :], sc == 0, sc == 31,
                                   [R_zs, R_tb], [R_ap])
                            dst = absb[:, bufi, tr, cc, :]
                            if cc % 2 == 0:
                                P.add("act", lambda e, dst=dst, ap_=ap_: e.activation(out=dst, in_=ap_, func=AF.Copy),
                                      reads=[R_ap], writes=[R_ab[bufi][tr]])
                            else:
                                P.add("dve", lambda e, dst=dst, ap_=ap_: e.tensor_copy(out=dst, in_=ap_),
                                      reads=[R_ap], writes=[R_ab[bufi][tr]])
                    for gl in range(2):
                        for mh in range(2):
                            yp, R_yp = psum()
                            n = 0
                            for tr in range(2):
                                for c2 in range(2):
                                    mm(yp, ctab[:, tr, c2, mh * 128:(mh + 1) * 128], absb[:, bufi, tr, gl * 2 + c2, :],
                                       n == 0, n == 3, [R_ctab, R_ab[bufi][tr]], [R_yp])
                                    n += 1
                            yb = yti % 2
                            yti += 1
                            P.add("act", lambda e, yp=yp, yb=yb: e.activation(out=ytb[:, yb, :], in_=yp, func=AF.Copy),
                                  reads=[R_yp], writes=[R_ytb[yb]])
                            mcg = ch * 4 + gl * 2 + mh
                            P.add("sp", lambda e, yb=yb, mcg=mcg, kb=kb: e.dma_start(
                                out=yts[:, mcg, kb * 512:(kb + 1) * 512], in_=ytb[:, yb, :]),
                                reads=[R_ytb[yb]], writes=[R_yts], kind="dma")
            P.barrier()

        def phase3():
            PH[0] = 3
            AR.off = base_off + DFT_BYTES
            xs = AR.alloc([4, D], F32)
            R_xs = [[Res(f"xs3{i}_{f}") for f in range(4)] for i in range(4)]
            hT = AR.alloc([16, T], BF16)
            R_hT = [[Res(f"hT3{i}_{q}") for q in range(4)] for i in range(4)]
            ss = AR.alloc([4], F32)
            rstd = AR.alloc([4], F32)
            hn = AR.alloc([2, D], BF16)
            ntmp = (ss, [Res(f"ss3{i}") for i in range(4)], rstd, [Res(f"rstd3{i}") for i in range(4)],
                    hn, [Res("hn30"), Res("hn31")])
            yt_t = AR.alloc([8, T], BF16)
            R_ytt = Res("yt_t")
            hs_t = AR.alloc([8, T], BF16)
            R_hst = Res("hs_t")
            top_end = AR.off
            AR.off = base_off
            ring = Ring(3, 12288)
            sg = AR.alloc([2, T], F32)
            sg_off = AR.last
            R_sg = [Res("sg30"), Res("sg31")]
            actb = AR.alloc([2, 2, T], BF16)
            R_actb = [Res("act30"), Res("act31")]
            hg = hs_t
            R_hg = R_hst
            mrg = AR.alloc([16, T], BF16)
            R_mrg = Res("mrg")
            gt = AR.alloc([2, T], F32)
            R_gt = [Res("gt0"), Res("gt1")]
            m1 = AR.alloc([2, T], F32)
            R_m1 = [Res("m10"), Res("m11")]
            scr = AR.alloc([4, T], F32)
            R_scr = [Res(f"scr{q}") for q in range(4)]
            g1, R_g1 = scr[:, 0:2, :], R_scr[0:2]
            g2, R_g2 = scr[:, 2:4, :], R_scr[2:4]
            m1a, R_m1a = scr, R_scr
            assert AR.off <= base_off + DFT_BYTES, AR.off

            def load_yh(t):
                P.add("sp", lambda e: e.dma_start(out=yt_t, in_=yts[:, :, t * T:(t + 1) * T]),
                      reads=[R_yts], writes=[R_ytt], kind="dma")
                P.add("sp", lambda e: e.dma_start(out=hs_t, in_=hss[:, :, t * T:(t + 1) * T]),
                      reads=[R_hss], writes=[R_hst], kind="dma")

            def load_x1(t, only=None):
                for i in (range(4) if only is None else [only]):
                    P.add("sp", lambda e, i=i: e.dma_start(
                        out=xs[:, i, :], in_=x1s[t * T + i * 128:t * T + (i + 1) * 128, :]),
                        reads=[R_x1s], writes=R_xs[i], kind="dma")

            P.bypass = True
            load_x1(0)
            load_yh(0)
            for t in range(NT):
                norm_transpose(xs, R_xs, 1, hT, R_hT, ntmp)
                P.bypass = False
                for u in range(2):
                    slot, R_slot = ring.load(win[4 + u], 8192)
                    for c4 in range(4):
                        mc = u * 4 + c4
                        b = mc % 2
                        zp, R_zp = psum()
                        for kc in range(16):
                            o = kc * 512 + c4 * 128
                            mm(zp, slot[:, o:o + 128], hT[:, kc, :], kc == 0, kc == 15, RH(R_hT, kc) + [R_slot], [R_zp])
                        P.add("act", lambda e, zp=zp, b=b: e.activation(out=g1[:, b, :], in_=zp, func=AF.Square),
                              reads=[R_zp], writes=[R_g1[b]])
                        P.add("dve", lambda e, b=b: e.tensor_scalar(out=g1[:, b, :], in0=g1[:, b, :], scalar1=0.044715,
                                                                    scalar2=1.0, op0=ALU.mult, op1=ALU.add),
                              reads=[R_g1[b]], writes=[R_g1[b]])
                        P.add("dve", lambda e, zp=zp, b=b: e.tensor_tensor(out=g1[:, b, :], in0=g1[:, b, :], in1=zp,
                                                                           op=ALU.mult),
                              reads=[R_g1[b], R_zp], writes=[R_g1[b]])
                        P.add("act", lambda e, b=b: e.activation(out=g2[:, b, :], in_=g1[:, b, :], func=AF.Sigmoid,
                                                                 scale=1.5957691216057308),
                              reads=[R_g1[b]], writes=[R_g2[b]])
                        P.add("dve", lambda e, zp=zp, b=b: e.tensor_tensor(out=g2[:, b, :], in0=g2[:, b, :], in1=zp,
                                                                           op=ALU.mult),
                              reads=[R_g2[b], R_zp], writes=[R_g2[b]])
                        P.add("dve", lambda e, mc=mc, b=b: e.tensor_tensor(out=hg[:, mc, :], in0=g2[:, b, :],
                                                                           in1=hs_t[:, mc, :], op=ALU.mult),
                              reads=[R_g2[b], R_hst], writes=[R_hg])
                for fq in range(4):
                    sl_pa, R_pa = ring.load(pa[fq], 4096)
                    sl_ga, R_ga = ring.load(win[6 + fq], 8192)
                    units = None
                    for half in range(2):
                        if half == 1:
                            sl_pb, R_pb = ring.load(pb[fq], 4096)
                            sl_gb, R_gb = ring.load(win[10 + fq], 8192)
                        for fl in range(4):
                            fc = fq * 4 + fl
                            b = fc % 2
                            if half == 0:
                                slp, R_p, slg, R_g, src, R_src, boff = sl_pa, R_pa, sl_ga, R_ga, yt_t, R_ytt, 0
                            else:
                                slp, R_p, slg, R_g, src, R_src, boff = sl_pb, R_pb, sl_gb, R_gb, hg, R_hg, 16
                            yp, R_yp = psum()
                            for mc in range(8):
                                o = mc * 512 + fl * 128
                                mm(yp, slp[:, o:o + 128], src[:, mc, :], mc == 0, mc == 7, [R_p, R_src], [R_yp])
                            zp, R_zp = psum()
                            for kc in range(16):
                                o = kc * 512 + fl * 128
                                mm(zp, slg[:, o:o + 128], hT[:, kc, :], kc == 0, kc == 15, [R_g] + RH(R_hT, kc), [R_zp])
                            P.add("act", lambda e, zp=zp, b=b, half=half, bo=boff + fc: e.activation(
                                out=gt[:, b, :], in_=zp, func=AF.Sigmoid, bias=pv_sb[:, bo:bo + 1]),
                                reads=[R_zp, R_const], writes=[R_gt[b]])
                            if half == 0:
                                P.add("dve", lambda e, yp=yp, b=b, fc=fc: e.tensor_tensor(
                                    out=m1a[:, fc % 4, :], in0=gt[:, b, :], in1=yp, op=ALU.mult),
                                    reads=[R_gt[b], R_yp], writes=[R_m1a[fc % 4]])
                            else:
                                P.add("dve", lambda e, yp=yp, b=b: e.tensor_tensor(
                                    out=m1[:, b, :], in0=gt[:, b, :], in1=yp, op=ALU.mult),
                                    reads=[R_gt[b], R_yp], writes=[R_m1[b]])
                                P.add("dve", lambda e, b=b, fc=fc: e.tensor_tensor(
                                    out=mrg[:, fc, :], in0=m1[:, b, :], in1=m1a[:, fc % 4, :], op=ALU.add),
                                    reads=[R_m1[b], R_m1a[fc % 4]], writes=[R_mrg])
                if t + 1 < NT:
                    load_yh(t + 1)
                for fb in range(4):
                    slot, R_slot = ring.load(wo[fb], 8192)
                    for i in range(4):
                        dp, R_dp = psum()
                        for fc in range(16):
                            mm(dp, mrg[:, fc, i * 128:(i + 1) * 128], slot[:, fc * 512:(fc + 1) * 512],
                               fc == 0, fc == 15, [R_mrg, R_slot], [R_dp])
                        P.add("dve", lambda e, dp=dp, i=i, fb=fb: e.tensor_tensor(
                            out=xs[:, i, fb * 512:(fb + 1) * 512], in0=xs[:, i, fb * 512:(fb + 1) * 512], in1=dp,
                            op=ALU.add), reads=[R_dp, R_xs[i][fb]], writes=[R_xs[i][fb]])
                if DEBUG:
                    P.add("sp", lambda e, t=t: e.dma_start(
                        out=x2s[t * T:(t + 1) * T, :].rearrange("(i p) d -> p i d", p=128), in_=xs),
                        reads=sum(R_xs, []), writes=[R_none], kind="dma")
                norm_transpose(xs, R_xs, 2, hT, R_hT, ntmp)
                ffn(xs, R_xs, hT, R_hT, f2, ring, sg, R_sg, actb, R_actb)
                ss_, R_ss_, rstd_, R_rstd_, hn_, R_hn_ = ntmp
                gsb3, R_g3 = load_gamma(3)
                for i in range(4):
                    b = i % 2
                    P.add("act", lambda e, i=i, b=b: e.activation(out=hn_[:, b, :], in_=xs[:, i, :], func=AF.Square,
                                                                  accum_out=ss_[:, i:i + 1]),
                          reads=R_xs[i], writes=[R_hn_[b], R_ss_[i]])
                    P.add("act", lambda e, i=i: e.activation(out=rstd_[:, i:i + 1], in_=ss_[:, i:i + 1], func=AF.Sqrt,
                                                             scale=1.0 / D, bias=eps_sb),
                          reads=[R_ss_[i], R_const], writes=[R_rstd_[i]])
                    P.add("dve", lambda e, i=i: e.reciprocal(out=rstd_[:, i:i + 1], in_=rstd_[:, i:i + 1]),
                          reads=[R_rstd_[i]], writes=[R_rstd_[i]])
                    P.add("dve", lambda e, i=i, gsb3=gsb3: e.scalar_tensor_tensor(
                        out=xs[:, i, :], in0=xs[:, i, :], scalar=rstd_[:, i:i + 1], in1=gsb3,
                        op0=ALU.mult, op1=ALU.mult), reads=R_xs[i] + [R_rstd_[i], R_g3], writes=R_xs[i])
                for i in range(4):
                    P.add("sp", lambda e, t=t, i=i: e.dma_start(
                        out=out[t * T + i * 128:t * T + (i + 1) * 128, :], in_=xs[:, i, :]),
                        reads=R_xs[i], writes=[R_out], kind="dma")
                    if t + 1 < NT:
                        load_x1(t + 1, i)
            P.barrier(final=True)
            P.add("sp", lambda e: None, writes=[R_none])

        if 1 in PHASES:
            phase1()
        if 2 in PHASES:
            phase2a()
        if 3 in PHASES:
            phase2b()
        if 4 in PHASES:
            phase3()
        if 4 not in PHASES:
            P.barrier(final=True)
            P.add("sp", lambda e: None, writes=[R_none])

        P.finalize()
        sems = {}
        for eng in ("pe", "act", "dve"):
            for s in range(P.nmarks[eng] // SEG + 1):
                sems[("c", eng, s)] = es.enter_context(nc.semaphore(f"c_{eng}_{s}"))
        for q in ("sp", "pool"):
            for i in range(NDS):
                sems[("d", q, i)] = es.enter_context(nc.semaphore(f"d_{q}_{i}"))
        for i in range(P.ncc):
            sems[("d", "cc", i)] = es.enter_context(nc.semaphore(f"cc_{i}"))
        block = es.enter_context(nc.Block())

        @block.tensor
        def _(e):
            P.emit("pe", e, sems)

        @block.scalar
        def _(e):
            P.emit("act", e, sems)

        @block.vector
        def _(e):
            P.emit("dve", e, sems)

        @block.gpsimd
        def _(e):
            P.emit("pool", e, sems)

        @block.sync
        def _(e):
            P.emit("sp", e, sems)

    return nc


def _units_fm(w, cw):
    K, N = w.shape
    a = w.reshape(K // 128, 128, N // cw, cw).transpose(2, 1, 0, 3)
    return np.ascontiguousarray(a).reshape(N // cw, 128, (K // 128) * cw)


def _ffn_units(wg, wu, wd):
    g = _units_fm(wg, 256)
    u = _units_fm(wu, 256)
    d = wd.reshape(NG, 2, 128, D).transpose(0, 2, 1, 3).reshape(NG, 128, 2 * D)
    return np.ascontiguousarray(np.concatenate([g, u, d], axis=2))


def _dft_tables(hf):
    bf = ml_dtypes.bfloat16
    idx = np.arange(NT * 2 * T)
    t = idx // (2 * T)
    r = (idx // T) % 2
    i = idx % T
    loc = t * T + i
    s = np.where(r == 0, loc, SEQ - 1 - loc).astype(np.int64)
    kl = np.arange(NTOK)
    k = kl if hf == 0 else SEQ - 1 - kl
    ph = (np.outer(s, k) % SEQ).astype(np.float64) * (2.0 * np.pi / SEQ)
    tabs = np.empty((8, 128, 32 * 512), dtype=bf)
    for tr, fn in ((0, np.cos), (1, np.sin)):
        full = fn(ph).astype(np.float32)
        for kb in range(4):
            blk = full[:, kb * 512:(kb + 1) * 512].reshape(32, 128, 512).transpose(1, 0, 2)
            tabs[kb * 2 + tr] = np.ascontiguousarray(blk).reshape(128, 32 * 512).astype(bf)
    return tabs


def _dftc_table():
    bf = ml_dtypes.bfloat16
    c = np.arange(256)
    ph = (np.outer(c, c) % 256).astype(np.float64) * (2.0 * np.pi / 256)
    sc = 1.0 / math.sqrt(SEQ * 256.0)
    cc = (np.cos(ph) * sc).astype(np.float32)
    sn = (-np.sin(ph) * sc).astype(np.float32)
    tab = np.stack([cc.reshape(2, 128, 256), sn.reshape(2, 128, 256)], axis=0)
    return np.ascontiguousarray(tab.transpose(2, 0, 1, 3)).reshape(128, 1024).astype(bf)


_CACHE = {}


def kernel(x, ffn1_norm, ffn1_w_gate, ffn1_w_up, ffn1_w_down, mix_norm, w_in, b_gates,
           conv_w, conv_b, lru_wa, lru_ba, lru_wx, lru_bx, lru_lambda, proj_a, proj_b, w_out,
           ffn2_norm, ffn2_w_gate, ffn2_w_up, ffn2_w_down, final_norm):
    f32 = np.float32
    x = np.asarray(x, f32)
    A = lambda v: np.asarray(v, f32)
    f1 = _ffn_units(A(ffn1_w_gate)[0], A(ffn1_w_up)[0], A(ffn1_w_down)[0])
    f2 = _ffn_units(A(ffn2_w_gate)[0], A(ffn2_w_up)[0], A(ffn2_w_down)[0])
    win = _units_fm(A(w_in)[0], 512)
    pa_u = _units_fm(A(proj_a)[0], 512)
    pb_u = _units_fm(A(proj_b)[0], 512)
    wo_u = _units_fm(A(w_out)[0], 512)
    gam = np.concatenate([A(ffn1_norm)[0], A(mix_norm)[0], A(ffn2_norm)[0], A(final_norm)])
    gam = np.ascontiguousarray(np.broadcast_to(gam[None, :], (128, 4 * D)))
    ident = np.eye(128, dtype=f32).astype(ml_dtypes.bfloat16)
    dftc = _dftc_table()
    dfts = [_dft_tables(0), _dft_tables(1)]
    wa, wx = A(lru_wa)[0], A(lru_wx)[0]
    ba, bx, lam = A(lru_ba)[0], A(lru_bx)[0], A(lru_lambda)[0]
    cw, cb, bg = A(conv_w)[0], A(conv_b)[0], A(b_gates)[0]

    lruw_h, pv_h = [], []
    for hf in range(2):
        dirs = (0, 1) if hf == 0 else (1, 0)
        lw = np.empty((128, 2, 2, 8, 128), f32)
        for wi, w in enumerate((wa, wx)):
            for di, dg in enumerate(dirs):
                lw[:, wi, di] = w[dg].transpose(1, 0, 2)
        lruw_h.append(np.ascontiguousarray(lw.reshape(128, 4096)))
        pvv = np.zeros((128, NPV), f32)
        pvv[:, 0:32] = bg.reshape(2, 16, 128).transpose(2, 0, 1).reshape(128, 32)
        taps = np.zeros((5, 1024), f32)
        if hf == 0:
            taps[0:4] = cw
        else:
            taps[1:5] = cw[::-1]
        pvv[:, 32:72] = taps.reshape(5, 8, 128).transpose(2, 1, 0).reshape(128, 40)
        pvv[:, 72:80] = cb.reshape(8, 128).T
        for off, v in ((80, ba), (96, bx), (112, lam)):
            for di, dg in enumerate(dirs):
                pvv[:, off + di * 8: off + di * 8 + 8] = v[dg].T
        pvv[:, 128 + (1 - hf)] = 1.0
        pv_h.append(pvv)

    if "nc" not in _CACHE:
        _CACHE["nc"] = build_program()
    nc = _CACHE["nc"]
    in_maps = []
    for c in range(8):
        b, hf = c // 2, c % 2
        xc = x[b, :NTOK] if hf == 0 else x[b, SEQ - 1:NTOK - 1:-1]
        in_maps.append({
            "x": np.ascontiguousarray(xc), "f1": f1, "f2": f2, "win": win, "pa": pa_u, "pb": pb_u,
            "wo": wo_u, "lruw": lruw_h[hf], "gam": gam, "pv": pv_h[hf], "dfts": dfts[hf],
            "dftc": dftc, "ident": ident,
        })
    res = run_bass_kernel_spmd(nc, in_maps, core_ids=list(range(8)))
    _CACHE["last"] = res
    outp = np.empty((4, SEQ, D), f32)
    for c in range(8):
        b, hf = c // 2, c % 2
        o = res.results[c]["out"]
        if hf == 0:
            outp[b, :NTOK] = o
        else:
            outp[b, NTOK:] = o[::-1]
    return outp
```
